# Optimizing a Trainium2 kernel written in Bass

```python
import jax, jax.numpy as jnp
from jax import lax
import numpy as np

D_MODEL = 1024
BATCH = 8
SEQ = 2048
DEPTH = 1

CHUNK = 64
CONV_WIDTH = 31
CONV_DIM = D_MODEL
RWKV_DIM = D_MODEL
RWKV_HEAD = 64
RWKV_HEADS = RWKV_DIM // RWKV_HEAD
DECAY_LORA = 64
AAA_LORA = 64
GATE_LORA = 128
D_FF = 4 * D_MODEL
N_BRANCH = 2
RMS_EPS = 1e-6
LN_EPS = 1e-5
GN_EPS = 64e-5
RWKV_COLS = 3 * RWKV_DIM + DECAY_LORA + AAA_LORA + GATE_LORA
IN_COLS = 2 * CONV_DIM + RWKV_COLS + N_BRANCH * D_MODEL

kernel_name = 'hybrid_conformer_rwkv7_adaln_block'


def rms_norm(x, gain):
    xf = x.astype(jnp.float32)
    y = xf * lax.rsqrt(jnp.mean(xf * xf, axis=-1, keepdims=True) + RMS_EPS)
    return (y * gain.astype(jnp.float32)).astype(x.dtype)


def layer_norm(x, gain, bias):
    xf = x.astype(jnp.float32)
    mu = jnp.mean(xf, axis=-1, keepdims=True)
    var = jnp.mean(jnp.square(xf - mu), axis=-1, keepdims=True)
    y = (xf - mu) * lax.rsqrt(var + LN_EPS)
    return (y * gain.astype(jnp.float32) + bias.astype(jnp.float32)).astype(x.dtype)


def modulate(h, shift, scale):
    return h * (1.0 + scale[:, None, :]) + shift[:, None, :]


def conformer_conv(u, conv_w, conv_b, ln_g, ln_b, w_pw, b_pw):
    za, zb = jnp.split(u, 2, axis=-1)
    z = za * jax.nn.sigmoid(zb)
    z = lax.conv_general_dilated(
        z, conv_w[:, None, :].astype(z.dtype), window_strides=(1,),
        padding=[(CONV_WIDTH - 1, 0)],
        dimension_numbers=('NWC', 'WIO', 'NWC'),
        feature_group_count=CONV_DIM) + conv_b
    z = jax.nn.silu(layer_norm(z, ln_g, ln_b))
    return z @ w_pw + b_pw


def rwkv7_scan(r, w, k, v, a, b):
    bsz, _, h, n = r.shape

    def step(S, inp):
        r_t, w_t, k_t, v_t, a_t, b_t = inp
        Sa = jnp.einsum('bhij,bhj->bhi', S, a_t)
        S = S * w_t[:, :, None, :] + Sa[..., None] * b_t[:, :, None, :] + v_t[..., None] * k_t[:, :, None, :]
        y = jnp.einsum('bhij,bhj->bhi', S, r_t)
        return S, y

    xs = tuple(jnp.moveaxis(t, 1, 0) for t in (r, w, k, v, a, b))
    S0 = jnp.zeros((bsz, h, n, n), jnp.float32)
    _, ys = lax.scan(step, S0, xs)
    return jnp.moveaxis(ys, 0, 1)


def rwkv7_mix(u, mix, w0, w2, a0, a2, g2, k_k, k_a, r_k, gn_g, gn_b, w_o):
    bsz, t, _ = u.shape
    u_prev = jnp.pad(u, ((0, 0), (1, 0), (0, 0)))[:, :-1]
    u = u + mix * (u_prev - u)
    r, k, v, xw, xa, xg = jnp.split(
        u, [RWKV_DIM, 2 * RWKV_DIM, 3 * RWKV_DIM, 3 * RWKV_DIM + DECAY_LORA,
            3 * RWKV_DIM + DECAY_LORA + AAA_LORA], axis=-1)
    w_log = -jax.nn.softplus(-(w0 + jnp.tanh(xw) @ w2)) - 0.5
    decay = jnp.exp(-jnp.exp(w_log.astype(jnp.float32)))
    a = jax.nn.sigmoid(a0 + xa @ a2)
    g = jax.nn.sigmoid(xg) @ g2
    hs = (bsz, t, RWKV_HEADS, RWKV_HEAD)
    kk = (k * k_k).astype(jnp.float32).reshape(hs)
    kk = kk / jnp.maximum(jnp.linalg.norm(kk, axis=-1, keepdims=True), 1e-12)
    k = k * (1.0 + (a - 1.0) * k_a)
    r_h, k_h, v_h = r.reshape(hs), k.reshape(hs), v.reshape(hs)
    a_h = a.astype(jnp.float32).reshape(hs)
    y = rwkv7_scan(r_h.astype(jnp.float32), decay.reshape(hs), k_h.astype(jnp.float32),
                   v_h.astype(jnp.float32), -kk, kk * a_h)
    mu = jnp.mean(y, axis=-1, keepdims=True)
    var = jnp.mean(jnp.square(y - mu), axis=-1, keepdims=True)
    y = ((y - mu) * lax.rsqrt(var + GN_EPS)).reshape(bsz, t, RWKV_DIM)
    y = (y * gn_g.astype(jnp.float32) + gn_b.astype(jnp.float32)).astype(u.dtype)
    bonus = (jnp.sum(r_h * k_h * r_k, axis=-1, keepdims=True) * v_h).reshape(bsz, t, RWKV_DIM)
    return ((y + bonus) * g) @ w_o


def hybrid_layer(x, c, w_ada, b_ada, g_norm1, w_in, conv_w, conv_b, conv_ln_g, conv_ln_b,
                 w_conv_pw, b_conv_pw, rwkv_mix, rwkv_w0, rwkv_w2, rwkv_a0, rwkv_a2, rwkv_g2,
                 rwkv_k_k, rwkv_k_a, rwkv_r_k, rwkv_gn_g, rwkv_gn_b, w_rwkv_o, w_out,
                 g_norm2, w_ff1, w_ff2):
    mod = jax.nn.silu(c) @ w_ada + b_ada
    shift1, scale1, gate1, shift2, scale2, gate2 = jnp.split(mod, 6, axis=-1)
    h = modulate(rms_norm(x, g_norm1), shift1, scale1)
    u = h @ w_in
    u_conv, u_rwkv, u_gate = jnp.split(u, [2 * CONV_DIM, 2 * CONV_DIM + RWKV_COLS], axis=-1)
    y_conv = conformer_conv(u_conv, conv_w, conv_b, conv_ln_g, conv_ln_b, w_conv_pw, b_conv_pw)
    y_rwkv = rwkv7_mix(u_rwkv, rwkv_mix, rwkv_w0, rwkv_w2, rwkv_a0, rwkv_a2, rwkv_g2,
                       rwkv_k_k, rwkv_k_a, rwkv_r_k, rwkv_gn_g, rwkv_gn_b, w_rwkv_o)
    g_conv, g_rwkv = jnp.split(jax.nn.sigmoid(u_gate), N_BRANCH, axis=-1)
    merged = g_conv * y_conv + g_rwkv * y_rwkv
    x = x + gate1[:, None, :] * (merged @ w_out)
    h2 = modulate(rms_norm(x, g_norm2), shift2, scale2)
    ff = jnp.square(jax.nn.relu(h2 @ w_ff1)) @ w_ff2
    return x + gate2[:, None, :] * ff


def setup_inputs(seed: int = 0) -> dict:
    key = jax.random.key(seed)
    ks = iter(jax.random.split(key, 40))
    f32 = jnp.float32
    L, D = DEPTH, D_MODEL

    def nrm(shape, scale):
        return jax.random.normal(next(ks), shape, f32) * scale

    return {
        'x': nrm((BATCH, SEQ, D), 1.0),
        'c': nrm((BATCH, D), 1.0),
        'w_ada': nrm((L, D, 6 * D), 0.5 * D ** -0.5),
        'b_ada': nrm((L, 6 * D), 0.01),
        'g_norm1': 1.0 + nrm((L, D), 0.02),
        'w_in': nrm((L, D, IN_COLS), D ** -0.5),
        'conv_w': nrm((L, CONV_WIDTH, CONV_DIM), CONV_WIDTH ** -0.5),
        'conv_b': nrm((L, CONV_DIM), 0.01),
        'conv_ln_g': 1.0 + nrm((L, CONV_DIM), 0.02),
        'conv_ln_b': nrm((L, CONV_DIM), 0.01),
        'w_conv_pw': nrm((L, CONV_DIM, D), CONV_DIM ** -0.5),
        'b_conv_pw': nrm((L, D), 0.01),
        'rwkv_mix': jax.random.uniform(next(ks), (L, RWKV_COLS), f32),
        'rwkv_w0': jax.random.uniform(next(ks), (L, RWKV_DIM), f32, -6.0, 1.0),
        'rwkv_w2': nrm((L, DECAY_LORA, RWKV_DIM), 0.5 * DECAY_LORA ** -0.5),
        'rwkv_a0': nrm((L, RWKV_DIM), 0.5),
        'rwkv_a2': nrm((L, AAA_LORA, RWKV_DIM), AAA_LORA ** -0.5),
        'rwkv_g2': nrm((L, GATE_LORA, RWKV_DIM), GATE_LORA ** -0.5),
        'rwkv_k_k': 0.85 + nrm((L, RWKV_DIM), 0.05),
        'rwkv_k_a': 1.0 + nrm((L, RWKV_DIM), 0.05),
        'rwkv_r_k': nrm((L, RWKV_HEADS, RWKV_HEAD), 0.1),
        'rwkv_gn_g': 1.0 + nrm((L, RWKV_DIM), 0.02),
        'rwkv_gn_b': nrm((L, RWKV_DIM), 0.01),
        'w_rwkv_o': nrm((L, RWKV_DIM, D), RWKV_DIM ** -0.5),
        'w_out': nrm((L, D, D), D ** -0.5),
        'g_norm2': 1.0 + nrm((L, D), 0.02),
        'w_ff1': nrm((L, D, D_FF), D ** -0.5),
        'w_ff2': nrm((L, D_FF, D), D_FF ** -0.5),
        'g_final': 1.0 + nrm((D,), 0.02),
    }


def reference(x, c, w_ada, b_ada, g_norm1, w_in, conv_w, conv_b, conv_ln_g, conv_ln_b,
              w_conv_pw, b_conv_pw, rwkv_mix, rwkv_w0, rwkv_w2, rwkv_a0, rwkv_a2, rwkv_g2,
              rwkv_k_k, rwkv_k_a, rwkv_r_k, rwkv_gn_g, rwkv_gn_b, w_rwkv_o, w_out,
              g_norm2, w_ff1, w_ff2, g_final):
    for l in range(DEPTH):
        x = hybrid_layer(x, c, w_ada[l], b_ada[l], g_norm1[l], w_in[l], conv_w[l], conv_b[l],
                         conv_ln_g[l], conv_ln_b[l], w_conv_pw[l], b_conv_pw[l], rwkv_mix[l],
                         rwkv_w0[l], rwkv_w2[l], rwkv_a0[l], rwkv_a2[l], rwkv_g2[l], rwkv_k_k[l],
                         rwkv_k_a[l], rwkv_r_k[l], rwkv_gn_g[l], rwkv_gn_b[l], w_rwkv_o[l],
                         w_out[l], g_norm2[l], w_ff1[l], w_ff2[l])
    return rms_norm(x, g_final)
```

```python
import math
from contextlib import ExitStack

import numpy as np
import concourse.bass as bass
import concourse.mybir as mybir
from concourse.bass_utils import run_bass_kernel_spmd

F32 = mybir.dt.float32
BF16 = mybir.dt.bfloat16
ALU = mybir.AluOpType
AF = mybir.ActivationFunctionType

ENGS = ("pe", "act", "dve", "pool", "sp")

D = 1024
T = 2048
W = 256
NT = T // W
NCH = W // 64
PG = 4
NG = 2
C0 = math.exp(-0.5)
RMS_EPS = 1e-6
LN_EPS = 1e-5
GN_EPS = 64e-5

PV_G1, PV_G2, PV_GF, PV_CB, PV_LNG, PV_LNB, PV_BPW, PV_A0, PV_KK, PV_KA, PV_RK, PV_GNG, PV_GNB = [8 * i for i in range(13)]
PV_BADA = 104
PV_MIX = 152
PV_CW = 178
PV_EPS = 426
NPV = 430
CS_ID, CS_ONES, CS_BONES, CS_TRI = 0, 128, 256, 384
CS_SU4, CS_SL4, CS_IU4, CS_ID4 = 768, 1280, 1792, 2048
NCST = 2560

ENT_NAMES = (["conv%d" % i for i in range(4)] + ["gc0", "pw0", "gc1", "pw1", "lora"] +
             ["rkv%d" % p for p in range(8)] + ["gr0", "wo0", "gr1", "wo1", "wout0", "wout1"] +
             ["ff1_%d" % i for i in range(8)] + ["ff2_%d" % i for i in range(8)])
NE = len(ENT_NAMES)


class Op:
    __slots__ = ("eng", "fn", "deps", "needs_inc", "val", "sem", "is_dma", "chan", "idx")

    def __init__(self, eng, fn):
        self.eng = eng
        self.fn = fn
        self.deps = []
        self.needs_inc = False
        self.val = None
        self.sem = None
        self.is_dma = False
        self.chan = None
        self.idx = None


class Prog:
    def __init__(self, nc):
        self.nc = nc
        self.ops = {e: [] for e in ENGS}
        self.last_w = {}
        self.readers = {}
        self.chan_cnt = {}
        self.all_ops = []

    @staticmethod
    def _is_psum(t):
        return t == "psLN" or (isinstance(t, tuple) and t[0] in ("ps", "psT"))

    def _deps(self, op, reads, writes):
        writes = list(writes) + [t for t in reads if self._is_psum(t)]
        reads = [t for t in reads if not self._is_psum(t)]
        deps = set()
        for t in reads:
            w = self.last_w.get(t)
            if w is not None:
                deps.add(w)
        for t in writes:
            w = self.last_w.get(t)
            if w is not None:
                if not (op.is_dma and w.is_dma and w.chan == op.chan):
                    deps.add(w)
            for r in self.readers.get(t, ()):
                deps.add(r)
        deps.discard(op)
        if op.eng == "pe":
            deps = {d for d in deps if d.eng != "pe" or d.is_dma}
        for t in writes:
            self.last_w[t] = op
            self.readers[t] = set()
        for t in reads:
            self.readers.setdefault(t, set()).add(op)
        op.deps = list(deps)

    def add(self, eng, fn, reads=(), writes=()):
        op = Op(eng, fn)
        self._deps(op, reads, writes)
        op.idx = len(self.all_ops)
        self.all_ops.append(op)
        self.ops[eng].append(op)
        return op

    def dma(self, eng, out, in_, reads=(), writes=(), chan=None):
        op = Op(eng, lambda e, out=out, in_=in_: e.dma_start(out=out, in_=in_))
        op.is_dma = True
        op.chan = chan if chan is not None else (writes[0] if writes else ("rd", reads[0]))
        self.chan_cnt[op.chan] = self.chan_cnt.get(op.chan, 0) + 1
        op.val = 16 * self.chan_cnt[op.chan]
        self._deps(op, reads, writes)
        op.idx = len(self.all_ops)
        self.all_ops.append(op)
        self.ops[eng].append(op)
        return op

    def wait_ops(self, eng, ops):
        op = Op(eng, None)
        op.deps = list(ops)
        op.idx = len(self.all_ops)
        self.all_ops.append(op)
        self.ops[eng].append(op)
        return op

    def emit(self):
        nc = self.nc
        for op in self.all_ops:
            for d in op.deps:
                if not d.is_dma:
                    d.needs_inc = True
        with ExitStack() as es:
            esem = {e: es.enter_context(nc.semaphore("prog_" + e)) for e in ENGS}
            csem = {}
            for i, c in enumerate(self.chan_cnt):
                csem[c] = es.enter_context(nc.semaphore("ch%d" % i))
            for e in ENGS:
                cnt = 0
                for op in self.ops[e]:
                    if op.is_dma:
                        op.sem = csem[op.chan]
                    else:
                        op.sem = esem[e]
                        if op.needs_inc:
                            cnt += 1
                            op.val = cnt
            block = es.enter_context(nc.Block())

            def run(e, engine):
                waited = {}
                for op in self.ops[e]:
                    need = {}
                    for d in op.deps:
                        k = id(d.sem)
                        if waited.get(k, 0) >= d.val:
                            continue
                        if k not in need or need[k][1] < d.val:
                            need[k] = (d.sem, d.val)
                    for k, (s, v) in need.items():
                        engine.wait_ge(s, v)
                        waited[k] = v
                    if op.fn is None:
                        continue
                    ins = op.fn(engine)
                    if op.is_dma:
                        ins.then_inc(op.sem, 16)
                    elif op.needs_inc:
                        ins.then_inc(op.sem, 1)

            @block.tensor
            def _(eng):
                run("pe", eng)

            @block.scalar
            def _(eng):
                run("act", eng)

            @block.vector
            def _(eng):
                run("dve", eng)

            @block.gpsimd
            def _(eng):
                run("pool", eng)

            @block.sync
            def _(eng):
                run("sp", eng)


def MMS(lst):
    def f(e):
        ins = None
        for (o, l, r, s, t) in lst:
            ins = e.matmul(o, lhsT=l, rhs=r, start=s, stop=t)
        return ins
    return f


def TRS(lst):
    def f(e):
        ins = None
        for (o, i, idn) in lst:
            ins = e.transpose(out=o, in_=i, identity=idn)
        return ins
    return f


def ACTF(out, in_, func, bias=None, scale=None):
    kw = {}
    if bias is not None:
        kw["bias"] = bias
    if scale is not None:
        kw["scale"] = scale
    return lambda e: e.activation(out=out, in_=in_, func=func, **kw)


def TT(out, in0, in1, op):
    return lambda e: e.tensor_tensor(out=out, in0=in0, in1=in1, op=op)


def TS(out, in0, s1, s2, op0, op1=None):
    if op1 is None:
        return lambda e: e.tensor_scalar(out=out, in0=in0, scalar1=s1, scalar2=None, op0=op0)
    return lambda e: e.tensor_scalar(out=out, in0=in0, scalar1=s1, scalar2=s2, op0=op0, op1=op1)


def STT(out, in0, scalar, in1, op0, op1):
    return lambda e: e.scalar_tensor_tensor(out=out, in0=in0, scalar=scalar, in1=in1, op0=op0, op1=op1)


def CP(out, in_):
    return lambda e: e.tensor_copy(out=out, in_=in_)


def RECIP(out, in_):
    return lambda e: e.reciprocal(out=out, in_=in_)


def MSET(ap, v):
    return lambda e: e.memset(ap, v)


def SEQ(*fns):
    def f(e):
        ins = None
        for fn in fns:
            ins = fn(e)
        return ins
    return f


def build_program(nt_run=NT, debug=False, stage=99, TEV="dve"):
    nc = bass.Bass("TRN2", target_bir_lowering=False)

    def dram(name, shape, kind="ExternalInput"):
        return nc.dram_tensor(name, shape, F32, kind=kind).ap()

    x_d = dram("x", [T, D])
    c_d = dram("cvec", [128, 8])
    pv_d = dram("pvec", [128, NPV])
    cst_d = dram("cst", [128, NCST])
    wada_d = dram("wada", [12, 128, 4096])
    wts_d = dram("wts", [NE, 128, 4096])
    smallw_d = dram("smallw", [3, 128, 1024])
    w0_d = dram("w0row", [1, 1024])
    out_d = dram("out", [T, D], kind="ExternalOutput")
    dbg = {}

    def dbg_out(name, shape):
        dbg[name] = dram("dbg_" + name, shape, kind="ExternalOutput")
        return dbg[name]

    def sb(name, shape, dt=F32):
        return nc.alloc_sbuf_tensor("s_" + name, shape, dt)

    P = Prog(nc)

    cstf = sb("cstf", [128, NCST])
    pv = sb("pv", [128, NPV])
    cv = sb("cv", [128, 8])
    ident_b = sb("ident_b", [128, 128], BF16)
    ones_b = sb("ones_b", [128, 128], BF16)
    bones_b = sb("bones_b", [128, 128], BF16)
    id4_b = sb("id4_b", [128, NCH, 128], BF16)
    w2f = sb("w2f", [128, 1024])
    a2b = sb("a2b", [128, 1024], BF16)
    g2b = sb("g2b", [128, 1024], BF16)
    w0r = sb("w0r", [1, 1024])
    onesrow = sb("onesrow", [1, 128])
    omm = sb("omm", [128, 26])
    omka = sb("omka", [128, 8])
    scb = sb("scb", [128, 8], BF16)
    modt = sb("modt", [128, 48])
    gs1 = sb("gs1", [128, 8])
    gs2 = sb("gs2", [128, 8])

    x = sb("x", [128, 8, W])
    h = sb("h", [128, 8, W + 2], BF16)
    hlast = sb("hlast", [128, 8, 2], BF16)
    NS = 3
    wslot = [sb("wslot%d" % i, [128, 4096], BF16) for i in range(NS)]
    zbuf = sb("zbuf", [128, 8, 30 + W])
    zs = sb("zs", [128, 8, W], BF16)
    m1 = sb("m1", [128, 8, W])
    merged = sb("merged", [128, 8, W], BF16)
    xwa = sb("xwa", [128, W])
    txw = sb("txw", [128, W])
    xab = sb("xab", [128, W], BF16)
    xgs = sb("xgs", [128, W])
    sxg = sb("sxg", [128, W], BF16)

    arenaA = sb("arenaA", [128, 30 * 1024 // 4])
    baseA = nc.lookup_mloc(arenaA).addr
    SZB = 36 * 1024
    arenaB = sb("arenaB", [128, SZB // 4])
    baseB = nc.lookup_mloc(arenaB).addr
    GR = 512

    class Buf:
        def __init__(self, name, shape, dt, arena=None, off=None):
            es = 2 if dt == BF16 else 4
            self.name = name
            n = 1
            for s in shape[1:]:
                n *= s
            self.bytes = n * es
            self.nparts = shape[1] if len(shape) > 2 else 1
            self.pbytes = self.bytes // self.nparts
            self.arena = arena
            if arena is None:
                self.t = nc.alloc_sbuf_tensor("s_" + name, shape, dt)
                self.off = None
            else:
                base = baseA if arena == "A" else baseB
                lim = 30 * 1024 if arena == "A" else SZB
                assert off % 32 == 0 and off + self.bytes <= lim, (name, off, self.bytes)
                self.t = nc.alloc_sbuf_tensor_at("s_" + name, shape, dt, offset=base + off)
                self.off = off

        def tok(self, k=None, k1=None):
            if self.arena is None:
                if k is None:
                    return [(self.name, i) for i in range(self.nparts)]
                return [(self.name, i) for i in range(k, (k1 if k1 is not None else k + 1))]
            if k is None:
                lo, hi = self.off, self.off + self.bytes
            else:
                lo = self.off + k * self.pbytes
                hi = self.off + (k1 if k1 is not None else k + 1) * self.pbytes
            return [("ar" + self.arena, g) for g in range(lo // GR, (hi - 1) // GR + 1)]

    offA = [0]

    def bufA(name, shape, dt):
        b = Buf(name, shape, dt, "A", offA[0])
        offA[0] += (b.bytes + 31) // 32 * 32
        return b

    BDat = bufA("BDat", [128, PG * NCH, 128], BF16)
    rtb = bufA("rtb", [128, PG, W], BF16)
    BDV = bufA("BDV", [128, PG * NCH, 128], BF16)
    BDBh = bufA("BDBh", [128, PG * NCH, 128], BF16)
    BDKh = bufA("BDKh", [128, PG * NCH, 128], BF16)
    Ttb = bufA("Ttb", [128, PG * NCH, 128], BF16)
    AakT = bufA("AakT", [128, PG * NCH, 128], BF16)
    ArbT = bufA("ArbT", [128, PG * NCH, 64], BF16)
    ArkT = bufA("ArkT", [128, PG * NCH, 64], BF16)
    sizeA_scan = offA[0]
    offA[0] = 0
    fbuf = bufA("fbuf", [128, 32, W], BF16)

    offB = [0]

    def bufB(name, shape, dt, at=None):
        if at is not None:
            offB[0] = at
        b = Buf(name, shape, dt, "B", offB[0])
        offB[0] += (b.bytes + 31) // 32 * 32
        return b

    xin = [bufB("xin%d" % i, [128, 1024], F32) for i in range(2)]
    xout = [bufB("xout%d" % i, [128, 1024], F32) for i in range(2)]
    nsq = [bufB("nsq%d" % i, [128, W], BF16) for i in range(2)]
    nrt = bufB("nrt", [128, W], F32)
    nrs = bufB("nrs", [128, W], F32)
    ntmp = [bufB("ntmp%d" % i, [128, W], F32) for i in range(2)]
    offB[0] = 0
    zc = bufB("zc", [128, 8, W], F32)
    zsq = [bufB("zsq%d" % i, [128, W], F32) for i in range(2)]
    acca = [bufB("acca%d" % i, [128, W], F32) for i in range(2)]
    accb = [bufB("accb%d" % i, [128, W], F32) for i in range(2)]
    sgb = [bufB("sgb%d" % i, [128, W], F32) for i in range(2)]
    ptap = [bufB("ptap%d" % i, [128, W], F32) for i in range(2)]
    lnm = bufB("lnm", [128, W], F32)
    lne = bufB("lne", [128, W], F32)
    lnv = bufB("lnv", [128, W], F32)
    lnr = bufB("lnr", [128, W], F32)
    lnn = bufB("lnn", [128, W], F32)
    lnt = [bufB("lnt%d" % i, [128, W], F32) for i in range(2)]
    gct = [bufB("gct%d" % i, [128, W], F32) for i in range(2)]
    assert offB[0] <= SZB, offB[0]
    offB[0] = 0
    tq = [bufB("tq%d" % i, [128, W], F32) for i in range(2)]
    r_t = bufB("r_t", [128, W], F32)
    k_t = bufB("k_t", [128, W], F32)
    v_t = bufB("v_t", [128, W], F32)
    al = bufB("al", [128, W], F32)
    sgt = [bufB("sgt%d" % i, [128, 128], F32) for i in range(2)]
    eg = bufB("eg", [128, W], F32)
    ege = bufB("ege", [128, W], F32)
    egr = bufB("egr", [128, W], F32)
    egi = bufB("egi", [128, W], F32)
    kk2 = bufB("kk2", [128, W], BF16)
    nrm = bufB("nrm", [128, W], F32)
    rn = bufB("rn", [128, W], F32)
    kkn = bufB("kkn", [128, W], F32)
    fk = bufB("fk", [128, W], F32)
    knew = bufB("knew", [128, W], F32)
    rk = bufB("rk", [128, W], F32)
    rkr = bufB("rkr", [128, W], BF16)
    kka = bufB("kka", [128, W], F32)
    BDbt = bufB("BDbt", [128, NCH, 128], BF16)
    BDkt = bufB("BDkt", [128, NCH, 128], BF16)
    BFbh = bufB("BFbh", [128, NCH, 128], BF16)
    BFkh = bufB("BFkh", [128, NCH, 128], BF16)
    BFv = bufB("BFv", [128, NCH, 128], BF16)
    Mb = [bufB("Mb%d" % i, [128, NCH, 128], BF16) for i in range(2)]
    Ab = [bufB("Ab%d" % i, [128, NCH, 128], BF16) for i in range(2)]
    Xb = [bufB("Xb%d" % i, [128, NCH, 128], BF16) for i in range(2)]
    assert offB[0] <= SZB, offB[0]
    offB[0] = 0
    ysq = bufB("ysq", [128, W], F32)
    gmean = bufB("gmean", [128, W], F32)
    gmsq = bufB("gmsq", [128, W], F32)
    gvar = bufB("gvar", [128, W], F32)
    grs = bufB("grs", [128, W], F32)
    gd = bufB("gd", [128, W], F32)
    gyn = bufB("gyn", [128, W], F32)
    grt = [bufB("grt%d" % i, [128, W], F32) for i in range(2)]
    mtmp = [bufB("mtmp%d" % i, [128, W], F32) for i in range(2)]
    assert offB[0] <= SZB, offB[0]
    offB[0] = 22 * 1024
    rl = [bufB("rl%d" % i, [128, W], F32) for i in range(2)]
    assert offB[0] <= SZB, offB[0]

    Zb = sb("Zb", [128, PG, 128], BF16)
    Pb = sb("Pb", [128, PG, 128], BF16)
    Sf = sb("Sf", [128, 8, 128])
    Sb = sb("Sb", [128, 8, 128], BF16)
    gC = sb("gC", [128, 8, NCH])
    ybuf = sb("ybuf", [128, 8, W])
    bonus = sb("bonus", [128, 8, W])
    gbuf = sb("gbuf", [128, 8, W], BF16)
    yb = sb("yb", [128, 8, W], BF16)
    dummy = sb("dummy", [128, 8])

    NROT = 6
    psb = [nc.alloc_psum_tensor("psb%d" % i, [128, 512], F32) for i in range(NROT)]
    psT = nc.alloc_psum_tensor("psT", [128, 2, 512], BF16)
    psLN = nc.alloc_psum_tensor("psLN", [128, 2, W], F32)
    rot = [0]

    def ps_next():
        i = rot[0] % NROT
        rot[0] += 1
        return psb[i], ("ps", i)

    trot = [0]

    def pst_next():
        i = trot[0] % 2
        trot[0] += 1
        return psT[:, i, :], ("psT", 0)

    def pvc(col):
        return pv[:, col:col + 1]

    eps_rms = pvc(PV_EPS)
    eps_ln = pvc(PV_EPS + 1)
    eps_gn = pvc(PV_EPS + 2)
    eps_kk = pvc(PV_EPS + 3)
    identf = cstf[:, CS_ID:CS_ID + 128]
    onesf = cstf[:, CS_ONES:CS_ONES + 128]
    bonesf = cstf[:, CS_BONES:CS_BONES + 128]
    tri3 = cstf[:, CS_TRI:CS_TRI + 384]

    def c4(base, n=128):
        return cstf[:, base:base + NCH * n].rearrange("p (c t) -> p c t", c=NCH)

    maskSU4 = c4(CS_SU4)
    maskSL4 = c4(CS_SL4)
    maskIU4 = c4(CS_IU4, 64)

    P.dma("sp", cstf[:], cst_d, writes=["cst"])
    P.dma("sp", pv[:], pv_d, writes=["pv"])
    P.dma("sp", cv[:], c_d, writes=["cv"])
    P.dma("sp", w0r[:], w0_d, writes=["w0r"])
    P.dma("sp", w2f[:], smallw_d[0], writes=["w2f"])
    P.dma("pool", a2b[:], smallw_d[1], writes=["a2b"])
    P.dma("pool", g2b[:], smallw_d[2], writes=["g2b"])
    P.add("dve", CP(ident_b[:], identf), reads=["cst"], writes=["ident_b"])
    P.add("dve", CP(ones_b[:], onesf), reads=["cst"], writes=["ones_b"])
    P.add("dve", CP(bones_b[:], bonesf), reads=["cst"], writes=["bones_b"])
    P.add("dve", CP(id4_b[:], c4(CS_ID4)), reads=["cst"], writes=["id4_b"])
    P.add("pool", MSET(onesrow[:], 1.0), writes=["onesrow"])
    P.add("pool", MSET(dummy[:], 0.0), writes=["dummy"])
    P.add("pool", MSET(Sf[:], 0.0), writes=["Sf"])
    P.add("pool", MSET(Sb[:], 0.0), writes=[("Sb", p) for p in range(8)])
    P.add("pool", MSET(zbuf[:], 0.0), writes=[("zbuf", k) for k in range(8)])
    P.add("pool", MSET(hlast[:], 0.0), writes=["hlast"])
    P.add("dve", TS(omm[:], pv[:, PV_MIX:PV_MIX + 26], -1.0, 1.0, ALU.mult, ALU.add), reads=["pv"], writes=["omm"])
    P.add("dve", TS(omka[:], pv[:, PV_KA:PV_KA + 8], -1.0, 1.0, ALU.mult, ALU.add), reads=["pv"], writes=["omka"])

    wstate = {"issued": 0}
    total_ent = nt_run * NE

    def w_issue_upto(n):
        while wstate["issued"] < min(n, total_ent):
            g = wstate["issued"]
            s = g % NS
            P.dma("pool", wslot[s][:], wts_d[g % NE], writes=[("ws", s)])
            wstate["issued"] += 1

    went = {"g": 0}

    def w_next(name):
        g = went["g"]
        assert ENT_NAMES[g % NE] == name, (ENT_NAMES[g % NE], name)
        w_issue_upto(g + 2)
        went["g"] += 1
        s = g % NS
        return wslot[s], ("ws", s)

    ADA = stage >= 0.2
    if ADA:
        sc_f = sb("sc_f", [128, 8])
        P.add("act", ACTF(sc_f[:], cv[:], AF.Sigmoid), reads=["cv"], writes=["sc_f"])
        P.add("dve", TT(scb[:], sc_f[:], cv[:], ALU.mult), reads=["sc_f", "cv"], writes=["scb"])
        psm, psm_t = ps_next()
        adaslot = [wslot[0], wslot[1]]
        for s in range(12):
            sl = adaslot[s % 2]
            slt = ("ws", s % 2)
            P.dma("pool", sl[:, :], wada_d[s], writes=[slt])
            v3 = sl[:, :].rearrange("p (k j) -> p k j", k=8)
            lst = []
            for o6 in range(4):
                gc = s * 4 + o6
                for k in range(8):
                    lst.append((psm[:, gc:gc + 1], v3[:, k, o6 * 128:(o6 + 1) * 128], scb[:, k:k + 1], k == 0, k == 7))
            P.add("pe", MMS(lst), reads=[slt, "scb"], writes=[psm_t])
        P.add("dve", TT(modt[:], psm[:, 0:48], pv[:, PV_BADA:PV_BADA + 48], ALU.add), reads=[psm_t, "pv"], writes=["modt"])
        P.add("dve", STT(gs1[:], modt[:, 8:16], 1.0, pv[:, PV_G1:PV_G1 + 8], ALU.add, ALU.mult), reads=["modt", "pv"], writes=["gs1"])
        P.add("dve", STT(gs2[:], modt[:, 32:40], 1.0, pv[:, PV_G2:PV_G2 + 8], ALU.add, ALU.mult), reads=["modt", "pv"], writes=["gs2"])

        if debug:
            P.dma("sp", dbg_out("modt", [128, 48]), modt[:], reads=["modt"], writes=["dbgo_modt"])
    def sh1(k):
        return modt[:, k:k + 1]

    def gate1(k):
        return modt[:, 16 + k:17 + k]

    def sh2(k):
        return modt[:, 24 + k:25 + k]

    def gate2(k):
        return modt[:, 40 + k:41 + k]

    out_dmas = []
    ev = [0]

    def evac_eng():
        ev[0] += 1
        return "act" if ev[0] % 2 else "dve"

    def evac_copy(eng, out, in_):
        if eng == "act":
            return lambda e: e.copy(out=out, in_=in_)
        return CP(out, in_)

    def load_x(t):
        for tb in range(2):
            P.dma("sp", xin[tb].t[:], x_d[t * W + tb * 128: t * W + (tb + 1) * 128, :], writes=xin[tb].tok())

    def rms_to_h(src_gs, src_sh, tagw):
        psn, psn_t = ps_next()
        for k in range(8):
            q = nsq[k % 2]
            P.add("act", ACTF(q.t[:], x[:, k, :], AF.Square), reads=[("x", k)], writes=q.tok())
            P.add("pe", MMS([(psn[:, 0:W], ones_b[:], q.t[:], k == 0, k == 7)]), reads=q.tok() + ["ones_b"], writes=[psn_t])
        P.add("act", ACTF(nrt.t[:], psn[:, 0:W], AF.Sqrt, bias=eps_rms, scale=1.0 / D), reads=[psn_t, "pv"], writes=nrt.tok())
        P.add("dve", RECIP(nrs.t[:], nrt.t[:]), reads=nrt.tok(), writes=nrs.tok())
        if debug and "nrs" not in dbg:
            P.dma("sp", dbg_out("nrs", [128, W]), nrs.t[:], reads=nrs.tok(), writes=["dbgo_nrs"])
        if stage < 0.8:
            return
        for k in range(8):
            q = ntmp[k % 2]
            P.add("dve", TT(q.t[:], x[:, k, :], nrs.t[:], ALU.mult), reads=[("x", k)] + nrs.tok(), writes=q.tok())
            P.add("act", ACTF(h[:, k, 2:W + 2], q.t[:], AF.Identity, bias=src_sh(k), scale=src_gs[:, k:k + 1]),
                  reads=q.tok() + ["modt", tagw], writes=[("h", k)])

    if stage >= 0.5:
        load_x(0)

    def tile_body(t):
        for tb in range(2):
            for kg in range(2):
                pt, pt_t = ps_next()
                lst = [(pt[:, j * 128:(j + 1) * 128], xin[tb].t[:, (kg * 4 + j) * 128:(kg * 4 + j + 1) * 128], identf)
                       for j in range(4)]
                P.add("pe", TRS(lst), reads=xin[tb].tok() + ["cst"], writes=[pt_t])
                en = evac_eng()
                P.add(en, evac_copy(en, x[:, kg * 4:kg * 4 + 4, tb * 128:(tb + 1) * 128],
                                    pt[:, 0:512].rearrange("p (k t) -> p k t", k=4)),
                      reads=[pt_t], writes=[("x", kg * 4 + j) for j in range(4)])
        if debug and t == 0:
            P.dma("sp", dbg_out("x0", [128, 8 * W]), x[:].rearrange("p k t -> p (k t)"), reads=[("x", k) for k in range(8)], writes=["dbgo_x0"])
        if stage < 0.6:
            return
        if stage >= 0.7:
            P.add("pool", CP(h[:, :, 0:2], hlast[:]), reads=["hlast"], writes=[("h", k) for k in range(8)])
        rms_to_h(gs1, sh1, "gs1")
        if stage < 0.9:
            return
        P.add("pool", CP(hlast[:], h[:, :, W:W + 2]), reads=[("h", k) for k in range(8)], writes=["hlast"])
        hreads = [("h", k) for k in range(8)]
        if debug and t == 0:
            P.add("dve", CP(m1[:], h[:, :, 2:W + 2]), reads=hreads, writes=[("m1", k) for k in range(8)])
            P.dma("sp", dbg_out("h", [128, 8 * W]), m1[:].rearrange("p k t -> p (k t)"), reads=[("m1", k) for k in range(8)], writes=["dbgo_h"])

        if stage < 2:
            return
        if t > 0:
            P.add("pool", CP(zbuf[:, :, 0:30], zbuf[:, :, W:W + 30]), reads=[("zbuf", k) for k in range(8)],
                  writes=[("zbuf", k) for k in range(8)])
        for cc in range(8):
            if cc % 2 == 0:
                sl, slt = w_next("conv%d" % (cc // 2))
                sl3 = sl[:, :].rearrange("p (k j) -> p k j", k=8)
            sub = (cc % 2) * 256
            pa, pa_t = ps_next()
            P.add("pe", MMS([(pa[:, 0:W], sl3[:, k, sub:sub + 128], h[:, k, 2:W + 2], k == 0, k == 7) for k in range(8)]),
                  reads=[slt] + hreads, writes=[pa_t])
            pbk, pb_t = ps_next()
            P.add("pe", MMS([(pbk[:, 0:W], sl3[:, k, sub + 128:sub + 256], h[:, k, 2:W + 2], k == 0, k == 7) for k in range(8)]),
                  reads=[slt] + hreads, writes=[pb_t])
            sg = sgb[cc % 2]
            P.add("act", ACTF(sg.t[:], pbk[:, 0:W], AF.Sigmoid), reads=[pb_t], writes=sg.tok())
            P.add("dve", TT(zbuf[:, cc, 30:30 + W], pa[:, 0:W], sg.t[:], ALU.mult), reads=[pa_t] + sg.tok(), writes=[("zbuf", cc)])
            aa = acca[cc % 2]
            ab = accb[cc % 2]

            def cw(j):
                return pvc(PV_CW + j * 8 + cc)
            P.add("dve", TS(aa.t[:], zbuf[:, cc, 0:W], cw(0), pvc(PV_CB + cc), ALU.mult, ALU.add),
                  reads=[("zbuf", cc), "pv"], writes=aa.tok())
            NDVE = 19
            for j in range(1, NDVE):
                P.add("dve", STT(aa.t[:], zbuf[:, cc, j:j + W], cw(j), aa.t[:], ALU.mult, ALU.add),
                      reads=[("zbuf", cc), "pv"] + aa.tok(), writes=aa.tok())
            P.add("pool", TS(ab.t[:], zbuf[:, cc, NDVE:NDVE + W], cw(NDVE), None, ALU.mult), reads=[("zbuf", cc), "pv"], writes=ab.tok())
            for j in range(NDVE + 1, 31):
                tp = ptap[j % 2]
                P.add("pool", TS(tp.t[:], zbuf[:, cc, j:j + W], cw(j), None, ALU.mult), reads=[("zbuf", cc), "pv"], writes=tp.tok())
                P.add("pool", TT(ab.t[:], ab.t[:], tp.t[:], ALU.add), reads=ab.tok() + tp.tok(), writes=ab.tok())
            P.add("dve", TT(zc.t[:, cc, :], aa.t[:], ab.t[:], ALU.add), reads=aa.tok() + ab.tok(), writes=zc.tok(cc))
            zq = zsq[cc % 2]
            P.add("act", ACTF(zq.t[:], zc.t[:, cc, :], AF.Square), reads=zc.tok(cc), writes=zq.tok())
            P.add("pe", MMS([(psLN[:, 0, :], onesf, zc.t[:, cc, :], cc == 0, cc == 7),
                             (psLN[:, 1, :], onesf, zq.t[:], False, cc == 7)]),
                  reads=zc.tok(cc) + zq.tok() + ["cst"], writes=["psLN"])
        P.add("act", ACTF(lnm.t[:], psLN[:, 0, :], AF.Identity, scale=1.0 / D), reads=["psLN"], writes=lnm.tok())
        P.add("pool", TT(lne.t[:], lnm.t[:], lnm.t[:], ALU.mult), reads=lnm.tok(), writes=lne.tok())
        P.add("dve", STT(lnv.t[:], psLN[:, 1, :], 1.0 / D, lne.t[:], ALU.mult, ALU.subtract), reads=["psLN"] + lne.tok(), writes=lnv.tok())
        P.add("act", ACTF(lnr.t[:], lnv.t[:], AF.Sqrt, bias=eps_ln, scale=1.0), reads=lnv.tok() + ["pv"], writes=lnr.tok())
        P.add("dve", RECIP(lnr.t[:], lnr.t[:]), reads=lnr.tok(), writes=lnr.tok())
        P.add("dve", STT(lnn.t[:], lnm.t[:], -1.0, lnr.t[:], ALU.mult, ALU.mult), reads=lnm.tok() + lnr.tok(), writes=lnn.tok())
        for cc in range(8):
            q = lnt[cc % 2]
            q2 = sgb[cc % 2]
            P.add("dve", TT(q.t[:], zc.t[:, cc, :], lnr.t[:], ALU.mult), reads=zc.tok(cc) + lnr.tok(), writes=q.tok())
            P.add("pool", TT(q.t[:], q.t[:], lnn.t[:], ALU.add), reads=q.tok() + lnn.tok(), writes=q.tok())
            P.add("dve", TS(q.t[:], q.t[:], pvc(PV_LNG + cc), pvc(PV_LNB + cc), ALU.mult, ALU.add), reads=q.tok() + ["pv"], writes=q.tok())
            P.add("act", ACTF(q2.t[:], q.t[:], AF.Sigmoid), reads=q.tok(), writes=q2.tok())
            P.add("dve", TT(zs[:, cc, :], q.t[:], q2.t[:], ALU.mult), reads=q.tok() + q2.tok(), writes=[("zs", cc)])
        if debug and t == 0:
            P.dma("sp", dbg_out("zc", [128, 8 * W]), zc.t[:].rearrange("p k t -> p (k t)"), reads=zc.tok(), writes=["dbgo_zc"])
        zsreads = [("zs", k) for k in range(8)]
        if debug and t == 0:
            P.dma("pool", dbg_out("zs", [128, 8 * W]), zs[:].rearrange("p k t -> p (k t)"), reads=zsreads, writes=["dbgo_zs"])
            P.dma("sp", dbg_out("lnr", [128, W]), lnr.t[:], reads=lnr.tok(), writes=["dbgo_lnr"])
            P.dma("sp", dbg_out("lnm", [128, W]), lnm.t[:], reads=lnm.tok(), writes=["dbgo_lnm"])
        for oc in range(8):
            if oc % 4 == 0:
                slg, slg_t = w_next("gc%d" % (oc // 4))
                slg3 = slg[:, :].rearrange("p (k j) -> p k j", k=8)
                slp, slp_t = w_next("pw%d" % (oc // 4))
                slp3 = slp[:, :].rearrange("p (k j) -> p k j", k=8)
            sub = (oc % 4) * 128
            pg_, pg_t = ps_next()
            P.add("pe", MMS([(pg_[:, 0:W], slg3[:, k, sub:sub + 128], h[:, k, 2:W + 2], k == 0, k == 7) for k in range(8)]),
                  reads=[slg_t] + hreads, writes=[pg_t])
            gq = gct[oc % 2]
            P.add("act", ACTF(gq.t[:], pg_[:, 0:W], AF.Sigmoid), reads=[pg_t], writes=gq.tok())
            py, py_t = ps_next()
            P.add("pe", MMS([(py[:, 0:W], slp3[:, k, sub:sub + 128], zs[:, k, :], k == 0, k == 7) for k in range(8)]),
                  reads=[slp_t] + zsreads, writes=[py_t])
            P.add("dve", STT(m1[:, oc, :], py[:, 0:W], pvc(PV_BPW + oc), gq.t[:], ALU.add, ALU.mult),
                  reads=[py_t, "pv"] + gq.tok(), writes=[("m1", oc)])
        if debug and t == 0:
            P.dma("sp", dbg_out("m1", [128, 8 * W]), m1[:].rearrange("p k t -> p (k t)"), reads=[("m1", k) for k in range(8)], writes=["dbgo_m1"])

        if stage < 3:
            return
        for b in (BDat, BDbt, BDkt, BFbh, BFkh, BFv):
            P.add("pool", MSET(b.t[:], 0.0), writes=b.tok())
        hx = [h[:, k, 0:W + 2] for k in range(8)]
        sl, slt = w_next("lora")
        sl3 = sl[:, 0:2048].rearrange("p (k j) -> p k j", k=8)
        for which in range(2):
            pl, pl_t = ps_next()
            P.add("pe", MMS([(pl[:, 0:W + 2], sl3[:, k, which * 128:(which + 1) * 128], hx[k], k == 0, k == 7) for k in range(8)]),
                  reads=[slt] + hreads, writes=[pl_t])
            tqq = tq[which]
            dst = xwa if which == 0 else xgs
            dtok = "xwa" if which == 0 else "xgs"
            P.add("act", ACTF(tqq.t[:], pl[:, 1:W + 1], AF.Identity, scale=pvc(PV_MIX + 24 + which)), reads=[pl_t, "pv"], writes=tqq.tok())
            P.add("dve", STT(dst[:], pl[:, 2:W + 2], omm[:, 24 + which:25 + which], tqq.t[:], ALU.mult, ALU.add),
                  reads=[pl_t, "omm"] + tqq.tok(), writes=[dtok])
        P.add("act", ACTF(txw[0:64, :], xwa[0:64, :], AF.Tanh), reads=["xwa"], writes=["txw"])
        P.add("dve", CP(xab[64:128, :], xwa[64:128, :]), reads=["xwa"], writes=["xab"])
        P.add("act", ACTF(sxg[:], xgs[:], AF.Sigmoid), reads=["xgs"], writes=["sxg"])

        if stage < 3.2:
            return
        for grp in range(NG):
            for pp in range(PG):
                p = grp * PG + pp
                sl, slt = w_next("rkv%d" % p)
                sl3 = sl[:, 0:3072].rearrange("p (k j) -> p k j", k=8)
                dsts = (r_t, k_t, v_t)
                for q in range(3):
                    pq, pq_t = ps_next()
                    P.add("pe", MMS([(pq[:, 0:W + 2], sl3[:, k, q * 128:(q + 1) * 128], hx[k], k == 0, k == 7) for k in range(8)]),
                          reads=[slt] + hreads, writes=[pq_t])
                    tqq = tq[q % 2]
                    mc = q * 8 + p
                    P.add("act", ACTF(tqq.t[:], pq[:, 1:W + 1], AF.Identity, scale=pvc(PV_MIX + mc)), reads=[pq_t, "pv"], writes=tqq.tok())
                    P.add("dve", STT(dsts[q].t[:], pq[:, 2:W + 2], omm[:, mc:mc + 1], tqq.t[:], ALU.mult, ALU.add),
                          reads=[pq_t, "omm"] + tqq.tok(), writes=dsts[q].tok())
                if stage < 3.25:
                    return
                pcol = slice(p * 128, (p + 1) * 128)
                pa, pa_t = ps_next()
                P.add("pe", MMS([(pa[:, 0:W], a2b[64:128, pcol], xab[64:128, :], True, True)]), reads=["a2b", "xab"], writes=[pa_t])
                P.add("act", ACTF(al.t[:], pa[:, 0:W], AF.Sigmoid, bias=pvc(PV_A0 + p)), reads=[pa_t, "pv"], writes=al.tok())
                pg_, pg_t = ps_next()
                P.add("pe", MMS([(pg_[:, 0:W], g2b[:, pcol], sxg[:], True, True)]), reads=["g2b", "sxg"], writes=[pg_t])
                P.add("act", evac_copy("act", gbuf[:, p, :], pg_[:, 0:W]), reads=[pg_t], writes=[("gbuf", p)])
                if stage < 3.3:
                    return
                for tb in range(2):
                    pz, pz_t = ps_next()
                    P.add("pe", MMS([(pz[:, 0:128], txw[0:64, tb * 128:(tb + 1) * 128], w2f[0:64, pcol], True, False),
                                     (pz[:, 0:128], onesrow[0:1, :], w0r[0:1, pcol], False, True)]),
                          reads=["txw", "w2f", "onesrow", "w0r"], writes=[pz_t])
                    sq_ = sgt[tb]
                    P.add("act", ACTF(sq_.t[:], pz[:, 0:128], AF.Sigmoid), reads=[pz_t], writes=sq_.tok())
                    pcs, pcs_t = ps_next()
                    P.add("pe", MMS([(pcs[:, 0:384], sq_.t[:], tri3, True, True)]), reads=sq_.tok() + ["cst"], writes=[pcs_t])
                    ts_ = slice(tb * 128, (tb + 1) * 128)
                    P.add("act", ACTF(eg.t[:, ts_], pcs[:, 0:128], AF.Exp, scale=-C0), reads=[pcs_t], writes=eg.tok())
                    P.add("act", ACTF(egi.t[:, ts_], pcs[:, 0:128], AF.Exp, scale=C0), reads=[pcs_t], writes=egi.tok())
                    P.add("act", ACTF(ege.t[:, ts_], pcs[:, 128:256], AF.Exp, scale=-C0), reads=[pcs_t], writes=ege.tok())
                    P.add("act", ACTF(egr.t[:, ts_], pcs[:, 256:384], AF.Exp, scale=-C0), reads=[pcs_t], writes=egr.tok())
                if stage < 3.35:
                    return
                P.add("pool", CP(gC[:, p, :], eg.t[:, 63:W:64]), reads=eg.tok(), writes=[("gC", p)])
                P.add("act", ACTF(kk2.t[:], k_t.t[:], AF.Square, scale=pvc(PV_KK + p)), reads=k_t.tok() + ["pv"], writes=kk2.tok())
                pss, pss_t = ps_next()
                P.add("pe", MMS([(pss[:, 0:W], bones_b[:], kk2.t[:], True, True)]), reads=kk2.tok() + ["bones_b"], writes=[pss_t])
                P.add("act", ACTF(nrm.t[:], pss[:, 0:W], AF.Sqrt, bias=eps_kk, scale=1.0), reads=[pss_t, "pv"], writes=nrm.tok())
                P.add("dve", RECIP(rn.t[:], nrm.t[:]), reads=nrm.tok(), writes=rn.tok())
                P.add("dve", STT(kkn.t[:], k_t.t[:], pvc(PV_KK + p), rn.t[:], ALU.mult, ALU.mult), reads=k_t.tok() + rn.tok() + ["pv"], writes=kkn.tok())
                P.add("dve", TS(fk.t[:], al.t[:], pvc(PV_KA + p), omka[:, p:p + 1], ALU.mult, ALU.add), reads=al.tok() + ["pv", "omka"], writes=fk.tok())
                P.add("pool", TT(knew.t[:], k_t.t[:], fk.t[:], ALU.mult), reads=k_t.tok() + fk.tok(), writes=knew.tok())
                P.add("pool", TT(rk.t[:], r_t.t[:], knew.t[:], ALU.mult), reads=r_t.tok() + knew.tok(), writes=rk.tok())
                P.add("act", ACTF(rkr.t[:], rk.t[:], AF.Identity, scale=pvc(PV_RK + p)), reads=rk.tok() + ["pv"], writes=rkr.tok())
                pbn, pbn_t = ps_next()
                P.add("pe", MMS([(pbn[:, 0:W], bones_b[:], rkr.t[:], True, True)]), reads=rkr.tok() + ["bones_b"], writes=[pbn_t])
                P.add("dve", TT(bonus[:, p, :], pbn[:, 0:W], v_t.t[:], ALU.mult), reads=[pbn_t] + v_t.tok(), writes=[("bonus", p)])
                P.add("pool", TT(kka.t[:], kkn.t[:], al.t[:], ALU.mult), reads=kkn.tok() + al.tok(), writes=kka.tok())
                if stage < 3.4:
                    return
                P.add("dve", TT(rtb.t[:, pp, :], r_t.t[:], eg.t[:], ALU.mult), reads=r_t.tok() + eg.tok(), writes=rtb.tok(pp))

                def v4(ap):
                    return ap.rearrange("p (c t) -> p c t", c=NCH)

                def bd_write(eng, dst, c0, mk, reads, wtoks):
                    fns = []
                    for hh in range(2):
                        ps_ = slice(hh * 64, (hh + 1) * 64)
                        fns.append(mk(dst[ps_, c0:c0 + NCH, hh * 64:(hh + 1) * 64], ps_))
                    P.add(eng, SEQ(*fns), reads=reads, writes=wtoks)

                bd_write("pool", BDkt.t, 0, lambda o, ps_: TT(o, v4(knew.t[ps_, :]), v4(egi.t[ps_, :]), ALU.mult),
                         knew.tok() + egi.tok(), BDkt.tok())
                bd_write("dve", BDat.t, pp * NCH, lambda o, ps_: STT(o, v4(kkn.t[ps_, :]), -1.0, v4(ege.t[ps_, :]), ALU.mult, ALU.mult),
                         kkn.tok() + ege.tok(), BDat.tok(pp * NCH, (pp + 1) * NCH))
                bd_write("pool", BDbt.t, 0, lambda o, ps_: TT(o, v4(kka.t[ps_, :]), v4(egi.t[ps_, :]), ALU.mult),
                         kka.tok() + egi.tok(), BDbt.tok())
                bd_write("pool", BFbh.t, 0, lambda o, ps_: TT(o, v4(kka.t[ps_, :]), v4(egr.t[ps_, :]), ALU.mult),
                         kka.tok() + egr.tok(), BFbh.tok())
                bd_write("dve", BFkh.t, 0, lambda o, ps_: TT(o, v4(knew.t[ps_, :]), v4(egr.t[ps_, :]), ALU.mult),
                         knew.tok() + egr.tok(), BFkh.tok())
                bd_write("act", BFv.t, 0, lambda o, ps_: (lambda e, o=o, i=v4(v_t.t[ps_, :]): e.copy(out=o, in_=i)),
                         v_t.tok(), BFv.tok())

                if stage < 3.5:
                    return
                cs4 = slice(pp * NCH, (pp + 1) * NCH)
                pM, pM_t = ps_next()
                pA, pA_t = ps_next()
                pK, pK_t = ps_next()
                pR, pR_t = ps_next()
                pM3 = pM[:, 0:512].rearrange("p (c t) -> p c t", c=NCH)
                pA3 = pA[:, 0:512].rearrange("p (c t) -> p c t", c=NCH)
                pK3 = pK[:, 0:512].rearrange("p (c t) -> p c t", c=NCH)
                pR4 = pR[:, 0:512].rearrange("p (a c t) -> p a c t", a=2, c=NCH)
                lst = []
                for c in range(NCH):
                    atc = BDat.t[:, pp * NCH + c, :]
                    btc = BDbt.t[:, c, :]
                    ktc = BDkt.t[:, c, :]
                    rtc = rtb.t[:, pp, c * 64:(c + 1) * 64]
                    lst += [(pM3[:, c, :], btc, atc, True, True), (pA3[:, c, :], atc, btc, True, True),
                            (pK3[:, c, :], ktc, atc, True, True), (pR4[:, 0, c, :], btc, rtc, True, True),
                            (pR4[:, 1, c, :], ktc, rtc, True, True)]
                P.add("pe", MMS(lst), reads=BDat.tok(pp * NCH, (pp + 1) * NCH) + BDbt.tok() + BDkt.tok() + rtb.tok(pp),
                      writes=[pM_t, pA_t, pK_t, pR_t])
                if stage < 3.52:
                    return
                P.add("dve", TT(Mb[0].t[:], pM3, maskSU4, ALU.mult), reads=[pM_t, "cst"], writes=Mb[0].tok())
                if stage < 3.53:
                    return
                P.add("dve", TT(Ab[0].t[:], pA3, maskSL4, ALU.mult), reads=[pA_t, "cst"], writes=Ab[0].tok())
                P.add("dve", TT(AakT.t[:, cs4, :], pK3, maskSU4, ALU.mult), reads=[pK_t, "cst"], writes=AakT.tok(pp * NCH, (pp + 1) * NCH))
                if stage < 3.54:
                    return
                P.add("dve", TT(ArbT.t[:, cs4, :], pR4[:, 0], maskIU4, ALU.mult), reads=[pR_t, "cst"], writes=ArbT.tok(pp * NCH, (pp + 1) * NCH))
                P.add("dve", TT(ArkT.t[:, cs4, :], pR4[:, 1], maskIU4, ALU.mult), reads=[pR_t, "cst"], writes=ArkT.tok(pp * NCH, (pp + 1) * NCH))
                if stage < 3.55:
                    return
                P.add("pool", TT(Xb[0].t[:], Mb[0].t[:], id4_b[:], ALU.add), reads=Mb[0].tok() + ["id4_b"], writes=Xb[0].tok())
                if stage < 3.6:
                    return
                for (src, dstb) in ((BFbh, BDBh), (BFkh, BDKh), (BFv, BDV)):
                    ptt, ptt_t = pst_next()
                    ptt3 = ptt.rearrange("p (c t) -> p c t", c=NCH)
                    P.add("pe", TRS([(ptt3[:, c, :], src.t[:, c, :], ident_b[:]) for c in range(NCH)]),
                          reads=src.tok() + ["ident_b"], writes=[ptt_t])
                    if stage < 3.62:
                        return
                    P.add(TEV, evac_copy(TEV, dstb.t[:, cs4, :], ptt3), reads=[ptt_t], writes=dstb.tok(pp * NCH, (pp + 1) * NCH))
                if stage < 3.7:
                    return
                cur = 0
                for lv in range(1, 6):
                    nxt = 1 - cur
                    need_m = lv <= 4
                    pA2, pA2_t = ps_next()
                    pA23 = pA2[:, 0:512].rearrange("p (c t) -> p c t", c=NCH)
                    lst = [(pA23[:, c, :], Mb[cur].t[:, c, :], Ab[cur].t[:, c, :], True, True) for c in range(NCH)]
                    wr = [pA2_t]
                    if need_m:
                        pM2, pM2_t = ps_next()
                        pM23 = pM2[:, 0:512].rearrange("p (c t) -> p c t", c=NCH)
                        lst += [(pM23[:, c, :], Ab[cur].t[:, c, :], Mb[cur].t[:, c, :], True, True) for c in range(NCH)]
                        wr.append(pM2_t)
                    P.add("pe", MMS(lst), reads=Mb[cur].tok() + Ab[cur].tok(), writes=wr)
                    P.add("dve", CP(Ab[nxt].t[:], pA23), reads=[pA2_t], writes=Ab[nxt].tok())
                    if need_m:
                        P.add("act", evac_copy("act", Mb[nxt].t[:], pM23), reads=[pM2_t], writes=Mb[nxt].tok())
                    xc = (lv - 1) % 2
                    xn = 1 - xc
                    pX, pX_t = ps_next()
                    pX3 = pX[:, 0:512].rearrange("p (c t) -> p c t", c=NCH)
                    P.add("pe", MMS([(pX3[:, c, :], Ab[nxt].t[:, c, :], Xb[xc].t[:, c, :], True, True) for c in range(NCH)]),
                          reads=Ab[nxt].tok() + Xb[xc].tok(), writes=[pX_t])
                    if lv < 5:
                        P.add("dve", TT(Xb[xn].t[:], pX3, Xb[xc].t[:], ALU.add), reads=[pX_t] + Xb[xc].tok(), writes=Xb[xn].tok())
                    else:
                        P.add("dve", TT(Ttb.t[:, cs4, :], pX3, Xb[xc].t[:], ALU.add), reads=[pX_t] + Xb[xc].tok(),
                              writes=Ttb.tok(pp * NCH, (pp + 1) * NCH))
                    cur = nxt

            if stage < 3.8:
                return
            p0 = grp * PG
            for c in range(NCH):
                idx = [pp * NCH + c for pp in range(PG)]
                sbr = [("Sb", p0 + pp) for pp in range(PG)]
                pZ, pZ_t = ps_next()
                pZ3 = pZ[:, 0:512].rearrange("p (a t) -> p a t", a=PG)
                lst = []
                rd = list(sbr)
                for pp in range(PG):
                    lst.append((pZ3[:, pp, :], AakT.t[:, idx[pp], :], BDV.t[:, idx[pp], :], True, False))
                    lst.append((pZ3[:, pp, :], BDat.t[:, idx[pp], :], Sb[:, p0 + pp, :], False, True))
                    rd += AakT.tok(idx[pp]) + BDV.tok(idx[pp]) + BDat.tok(idx[pp])
                P.add("pe", MMS(lst), reads=rd, writes=[pZ_t])
                P.add("act", evac_copy("act", Zb[:], pZ3), reads=[pZ_t], writes=["Zb"])
                pP, pP_t = ps_next()
                pP3 = pP[:, 0:512].rearrange("p (a t) -> p a t", a=PG)
                rd = ["Zb"]
                for pp in range(PG):
                    rd += Ttb.tok(idx[pp])
                P.add("pe", MMS([(pP3[:, pp, :], Ttb.t[:, idx[pp], :], Zb[:, pp, :], True, True) for pp in range(PG)]),
                      reads=rd, writes=[pP_t])
                P.add("dve", CP(Pb[:], pP3), reads=[pP_t], writes=["Pb"])
                pY, pY_t = ps_next()
                pY3 = pY[:, 0:256].rearrange("p (a t) -> p a t", a=PG)
                pS, pS_t = ps_next()
                pS3 = pS[:, 0:512].rearrange("p (a t) -> p a t", a=PG)
                lst = []
                rd = ["Pb"] + list(sbr)
                for pp in range(PG):
                    rtc = rtb.t[:, pp, c * 64:(c + 1) * 64]
                    lst.append((pY3[:, pp, :], Sb[:, p0 + pp, :], rtc, True, False))
                    lst.append((pY3[:, pp, :], Pb[:, pp, :], ArbT.t[:, idx[pp], :], False, False))
                    lst.append((pY3[:, pp, :], BDV.t[:, idx[pp], :], ArkT.t[:, idx[pp], :], False, True))
                    lst.append((pS3[:, pp, :], BDBh.t[:, idx[pp], :], Pb[:, pp, :], True, False))
                    lst.append((pS3[:, pp, :], BDKh.t[:, idx[pp], :], BDV.t[:, idx[pp], :], False, True))
                    rd += rtb.tok(pp) + ArbT.tok(idx[pp]) + ArkT.tok(idx[pp]) + BDV.tok(idx[pp]) + BDBh.tok(idx[pp]) + BDKh.tok(idx[pp])
                P.add("pe", MMS(lst), reads=rd, writes=[pY_t, pS_t])
                P.add("act", evac_copy("act", ybuf[:, p0:p0 + PG, c * 64:(c + 1) * 64], pY3), reads=[pY_t],
                      writes=[("ybuf", p0 + pp) for pp in range(PG)])
                fns = [STT(Sf[:, p0 + pp, :], Sf[:, p0 + pp, :], gC[:, p0 + pp, c:c + 1], pS3[:, pp, :], ALU.mult, ALU.add)
                       for pp in range(PG)]
                P.add("dve", SEQ(*fns), reads=[pS_t, "Sf"] + [("gC", p0 + pp) for pp in range(PG)], writes=["Sf"])
                P.add("act", evac_copy("act", Sb[:, p0:p0 + PG, :], Sf[:, p0:p0 + PG, :]), reads=["Sf"], writes=list(sbr))

            if stage < 3.9:
                return
            for pp in range(PG):
                p = grp * PG + pp
                yp = ybuf[:, p, :]
                P.add("act", ACTF(ysq.t[:], yp, AF.Square), reads=[("ybuf", p)], writes=ysq.tok())
                pgn, pgn_t = ps_next()
                P.add("pe", MMS([(pgn[:, 0:W], bonesf, yp, True, True), (pgn[:, W:2 * W], bonesf, ysq.t[:], True, True)]),
                      reads=[("ybuf", p), "cst"] + ysq.tok(), writes=[pgn_t])
                P.add("act", ACTF(gmean.t[:], pgn[:, 0:W], AF.Identity, scale=1.0 / 64), reads=[pgn_t], writes=gmean.tok())
                P.add("pool", TT(gmsq.t[:], gmean.t[:], gmean.t[:], ALU.mult), reads=gmean.tok(), writes=gmsq.tok())
                P.add("dve", STT(gvar.t[:], pgn[:, W:2 * W], 1.0 / 64, gmsq.t[:], ALU.mult, ALU.subtract), reads=[pgn_t] + gmsq.tok(), writes=gvar.tok())
                P.add("act", ACTF(grs.t[:], gvar.t[:], AF.Sqrt, bias=eps_gn, scale=1.0), reads=gvar.tok() + ["pv"], writes=grs.tok())
                P.add("dve", RECIP(grs.t[:], grs.t[:]), reads=grs.tok(), writes=grs.tok())
                P.add("pool", TT(gd.t[:], yp, gmean.t[:], ALU.subtract), reads=[("ybuf", p)] + gmean.tok(), writes=gd.tok())
                P.add("dve", TT(gd.t[:], gd.t[:], grs.t[:], ALU.mult), reads=gd.tok() + grs.tok(), writes=gd.tok())
                P.add("act", ACTF(gyn.t[:], gd.t[:], AF.Identity, bias=pvc(PV_GNB + p), scale=pvc(PV_GNG + p)), reads=gd.tok() + ["pv"], writes=gyn.tok())
                P.add("pool", TT(gyn.t[:], gyn.t[:], bonus[:, p, :], ALU.add), reads=gyn.tok() + [("bonus", p)], writes=gyn.tok())
                P.add("dve", TT(yb[:, p, :], gyn.t[:], gbuf[:, p, :], ALU.mult), reads=gyn.tok() + [("gbuf", p)], writes=[("yb", p)])
        if debug and t == 0:
            P.dma("sp", dbg_out("y", [128, 8 * W]), ybuf[:].rearrange("p k t -> p (k t)"), reads=[("ybuf", k) for k in range(8)], writes=["dbgo_y"])
            P.dma("sp", dbg_out("bonus", [128, 8 * W]), bonus[:].rearrange("p k t -> p (k t)"), reads=[("bonus", k) for k in range(8)], writes=["dbgo_bonus"])

        if stage < 5:
            return
        ybreads = [("yb", k) for k in range(8)]
        for oc in range(8):
            if oc % 4 == 0:
                slg, slg_t = w_next("gr%d" % (oc // 4))
                slg3 = slg[:, :].rearrange("p (k j) -> p k j", k=8)
                slo, slo_t = w_next("wo%d" % (oc // 4))
                slo3 = slo[:, :].rearrange("p (k j) -> p k j", k=8)
            sub = (oc % 4) * 128
            pg_, pg_t = ps_next()
            P.add("pe", MMS([(pg_[:, 0:W], slg3[:, k, sub:sub + 128], h[:, k, 2:W + 2], k == 0, k == 7) for k in range(8)]),
                  reads=[slg_t] + hreads, writes=[pg_t])
            gq = grt[oc % 2]
            P.add("act", ACTF(gq.t[:], pg_[:, 0:W], AF.Sigmoid), reads=[pg_t], writes=gq.tok())
            py, py_t = ps_next()
            P.add("pe", MMS([(py[:, 0:W], slo3[:, k, sub:sub + 128], yb[:, k, :], k == 0, k == 7) for k in range(8)]),
                  reads=[slo_t] + ybreads, writes=[py_t])
            mq = mtmp[oc % 2]
            P.add("dve", TT(mq.t[:], py[:, 0:W], gq.t[:], ALU.mult), reads=[py_t] + gq.tok(), writes=mq.tok())
            P.add("pool", TT(merged[:, oc, :], mq.t[:], m1[:, oc, :], ALU.add), reads=mq.tok() + [("m1", oc)], writes=[("merged", oc)])
        mreads = [("merged", k) for k in range(8)]
        for oc in range(8):
            if oc % 4 == 0:
                slw, slw_t = w_next("wout%d" % (oc // 4))
                slw3 = slw[:, :].rearrange("p (k j) -> p k j", k=8)
            sub = (oc % 4) * 128
            po, po_t = ps_next()
            P.add("pe", MMS([(po[:, 0:W], slw3[:, k, sub:sub + 128], merged[:, k, :], k == 0, k == 7) for k in range(8)]),
                  reads=[slw_t] + mreads, writes=[po_t])
            P.add("dve", STT(x[:, oc, :], po[:, 0:W], gate1(oc), x[:, oc, :], ALU.mult, ALU.add),
                  reads=[po_t, "modt", ("x", oc)], writes=[("x", oc)])
        if debug and t == 0:
            P.dma("sp", dbg_out("x1", [128, 8 * W]), x[:].rearrange("p k t -> p (k t)"), reads=[("x", k) for k in range(8)], writes=["dbgo_x1"])

        if stage < 6:
            return
        rms_to_h(gs2, sh2, "gs2")
        for fc in range(32):
            if fc % 4 == 0:
                slf, slf_t = w_next("ff1_%d" % (fc // 4))
                slf3 = slf[:, :].rearrange("p (k j) -> p k j", k=8)
            sub = (fc % 4) * 128
            pf, pf_t = ps_next()
            P.add("pe", MMS([(pf[:, 0:W], slf3[:, k, sub:sub + 128], h[:, k, 2:W + 2], k == 0, k == 7) for k in range(8)]),
                  reads=[slf_t] + hreads, writes=[pf_t])
            rq = rl[fc % 2]
            P.add("act", ACTF(rq.t[:], pf[:, 0:W], AF.Relu), reads=[pf_t], writes=rq.tok())
            P.add("pool", TT(fbuf.t[:, fc, :], rq.t[:], rq.t[:], ALU.mult), reads=rq.tok(), writes=fbuf.tok(fc))
        for oc in range(8):
            sl2, sl2_t = w_next("ff2_%d" % oc)
            sl23 = sl2[:, :].rearrange("p (k j) -> p k j", k=32)
            po, po_t = ps_next()
            P.add("pe", MMS([(po[:, 0:W], sl23[:, k, :], fbuf.t[:, k, :], k == 0, k == 31) for k in range(32)]),
                  reads=[sl2_t] + fbuf.tok(), writes=[po_t])
            P.add("dve", STT(x[:, oc, :], po[:, 0:W], gate2(oc), x[:, oc, :], ALU.mult, ALU.add),
                  reads=[po_t, "modt", ("x", oc)], writes=[("x", oc)])

        if stage < 7:
            return
        psn, psn_t = ps_next()
        for k in range(8):
            q = nsq[k % 2]
            P.add("act", ACTF(q.t[:], x[:, k, :], AF.Square), reads=[("x", k)], writes=q.tok())
            P.add("pe", MMS([(psn[:, 0:W], ones_b[:], q.t[:], k == 0, k == 7)]), reads=q.tok() + ["ones_b"], writes=[psn_t])
        P.add("act", ACTF(nrt.t[:], psn[:, 0:W], AF.Sqrt, bias=eps_rms, scale=1.0 / D), reads=[psn_t, "pv"], writes=nrt.tok())
        P.add("dve", RECIP(nrs.t[:], nrt.t[:]), reads=nrt.tok(), writes=nrs.tok())
        for k in range(8):
            P.add("dve", STT(m1[:, k, :], x[:, k, :], pvc(PV_GF + k), nrs.t[:], ALU.mult, ALU.mult),
                  reads=[("x", k), "pv"] + nrs.tok(), writes=[("m1", k)])
        if t + 1 < nt_run:
            load_x(t + 1)
        for tb in range(2):
            for kg in range(2):
                pt, pt_t = ps_next()
                lst = [(pt[:, j * 128:(j + 1) * 128], m1[:, kg * 4 + j, tb * 128:(tb + 1) * 128], identf) for j in range(4)]
                P.add("pe", TRS(lst), reads=[("m1", kg * 4 + j) for j in range(4)] + ["cst"], writes=[pt_t])
                en = evac_eng()
                P.add(en, evac_copy(en, xout[tb].t[:, kg * 512:(kg + 1) * 512], pt[:, 0:512]), reads=[pt_t], writes=xout[tb].tok())
            out_dmas.append(P.dma("sp", out_d[t * W + tb * 128: t * W + (tb + 1) * 128, :], xout[tb].t[:],
                                  reads=xout[tb].tok(), writes=[("out", t, tb)], chan=("outc", tb)))
    for t in range(nt_run):
        if stage >= 0.5:
            tile_body(t)
    P.wait_ops("sp", out_dmas + [op for op in P.all_ops if op.is_dma and op.eng == "sp" and op.chan is not None and str(op.chan).startswith("dbgo")] + [op for op in P.all_ops if op.is_dma and op.eng == "pool" and str(op.chan).startswith("dbgo")])
    P.emit()
    return nc, dbg


def _fm(v):
    return np.ascontiguousarray(np.asarray(v, np.float32).reshape(-1, 128).T)


def _tile_k(wsub):
    wsub = np.asarray(wsub, np.float32)
    kc = wsub.shape[0] // 128
    a = wsub.reshape(kc, 128, wsub.shape[1]).transpose(1, 0, 2).reshape(128, -1)
    out = np.zeros((128, 4096), np.float32)
    out[:, :a.shape[1]] = a
    return out


def _consts():
    cst = np.zeros((128, NCST), np.float32)
    idx = np.arange(128)
    hh = idx // 64
    ss = idx % 64
    cst[:, CS_ID:CS_ID + 128] = np.eye(128)
    cst[:, CS_ONES:CS_ONES + 128] = 1.0
    same = (hh[:, None] == hh[None, :])
    cst[:, CS_BONES:CS_BONES + 128] = same
    cst[:, CS_TRI:CS_TRI + 128] = same & (ss[:, None] <= ss[None, :])
    cst[:, CS_TRI + 128:CS_TRI + 256] = same & (ss[:, None] < ss[None, :])
    cst[:, CS_TRI + 256:CS_TRI + 384] = same & (ss[:, None] > ss[None, :])
    su = (same & (ss[:, None] < ss[None, :])).astype(np.float32)
    sl = (same & (ss[:, None] > ss[None, :])).astype(np.float32)
    iu = (ss[:, None] <= np.arange(64)[None, :]).astype(np.float32)
    cst[:, CS_SU4:CS_SU4 + 512] = np.tile(su, (1, 4))
    cst[:, CS_SL4:CS_SL4 + 512] = np.tile(sl, (1, 4))
    cst[:, CS_IU4:CS_IU4 + 256] = np.tile(iu, (1, 4))
    cst[:, CS_ID4:CS_ID4 + 512] = np.tile(np.eye(128, dtype=np.float32), (1, 4))
    return cst


def prepare_shared(inp):
    L = 0
    pvec = np.zeros((128, NPV), np.float32)
    for col, key in ((PV_G1, "g_norm1"), (PV_G2, "g_norm2"), (PV_CB, "conv_b"), (PV_LNG, "conv_ln_g"), (PV_LNB, "conv_ln_b"),
                     (PV_BPW, "b_conv_pw"), (PV_A0, "rwkv_a0"), (PV_KK, "rwkv_k_k"), (PV_KA, "rwkv_k_a"),
                     (PV_GNG, "rwkv_gn_g"), (PV_GNB, "rwkv_gn_b")):
        pvec[:, col:col + 8] = _fm(inp[key][L])
    pvec[:, PV_GF:PV_GF + 8] = _fm(inp["g_final"])
    pvec[:, PV_RK:PV_RK + 8] = _fm(np.asarray(inp["rwkv_r_k"][L]).reshape(-1))
    pvec[:, PV_BADA:PV_BADA + 48] = _fm(inp["b_ada"][L])
    pvec[:, PV_MIX:PV_MIX + 26] = _fm(inp["rwkv_mix"][L])
    cw = np.asarray(inp["conv_w"][L], np.float32)
    for j in range(31):
        pvec[:, PV_CW + j * 8:PV_CW + j * 8 + 8] = _fm(cw[j])
    pvec[:, PV_EPS] = RMS_EPS
    pvec[:, PV_EPS + 1] = LN_EPS
    pvec[:, PV_EPS + 2] = GN_EPS
    pvec[:, PV_EPS + 3] = 1e-24

    w_in = np.asarray(inp["w_in"][L], np.float32)
    ents = {}
    for i in range(4):
        cols = np.concatenate([np.arange(cc * 128, cc * 128 + 128) if z == 0 else np.arange(1024 + cc * 128, 1024 + cc * 128 + 128)
                               for cc in (2 * i, 2 * i + 1) for z in (0, 1)])
        ents["conv%d" % i] = _tile_k(w_in[:, cols])
    for i in range(2):
        ents["gc%d" % i] = _tile_k(w_in[:, 5376 + i * 512:5376 + (i + 1) * 512])
        ents["gr%d" % i] = _tile_k(w_in[:, 6400 + i * 512:6400 + (i + 1) * 512])
        ents["pw%d" % i] = _tile_k(np.asarray(inp["w_conv_pw"][L])[:, i * 512:(i + 1) * 512])
        ents["wo%d" % i] = _tile_k(np.asarray(inp["w_rwkv_o"][L])[:, i * 512:(i + 1) * 512])
        ents["wout%d" % i] = _tile_k(np.asarray(inp["w_out"][L])[:, i * 512:(i + 1) * 512])
    ents["lora"] = _tile_k(w_in[:, 5120:5376])
    for p in range(8):
        cols = np.concatenate([np.arange(2048 + q * 1024 + p * 128, 2048 + q * 1024 + p * 128 + 128) for q in range(3)])
        ents["rkv%d" % p] = _tile_k(w_in[:, cols])
    w1 = np.asarray(inp["w_ff1"][L], np.float32)
    w2 = np.asarray(inp["w_ff2"][L], np.float32)
    for i in range(8):
        ents["ff1_%d" % i] = _tile_k(w1[:, i * 512:(i + 1) * 512])
        ents["ff2_%d" % i] = _tile_k(w2[:, i * 128:(i + 1) * 128])
    wts = np.stack([ents[n] for n in ENT_NAMES])

    wada = np.asarray(inp["w_ada"][L], np.float32)
    wada_t = np.stack([_tile_k(wada[:, s * 512:(s + 1) * 512]) for s in range(12)])
    smallw = np.zeros((3, 128, 1024), np.float32)
    smallw[0, 0:64] = inp["rwkv_w2"][L]
    smallw[1, 64:128] = inp["rwkv_a2"][L]
    smallw[2] = inp["rwkv_g2"][L]
    w0row = np.asarray(inp["rwkv_w0"][L], np.float32).reshape(1, 1024)
    return {"pvec": pvec, "cst": _consts(), "wada": np.ascontiguousarray(wada_t), "wts": np.ascontiguousarray(wts),
            "smallw": smallw, "w0row": np.ascontiguousarray(w0row)}


def kernel(**inputs):
    shared = prepare_shared(inputs)
    x = np.asarray(inputs["x"], np.float32)
    c = np.asarray(inputs["c"], np.float32)
    nc, _ = build_program()
    in_maps = []
    for b in range(8):
        m = dict(shared)
        m["x"] = np.ascontiguousarray(x[b])
        m["cvec"] = _fm(c[b])
        in_maps.append(m)
    res = run_bass_kernel_spmd(nc, in_maps, core_ids=list(range(8)))
    return np.stack([np.asarray(r["out"], np.float32) for r in res.results], axis=0)
```
